# Optimizing a Trainium2 kernel written in Bass

```python
import math
import jax, jax.numpy as jnp
from jax import lax
import numpy as np

D_MODEL = 2048
BATCH = 8
SEQ = 4096
DEPTH = 4

N_EVEN = (DEPTH + 1) // 2
N_ODD = DEPTH // 2
N_V_RES = N_ODD - 1
MIX_WIDTH = D_MODEL
DN_HEAD_DIM = 128
DN_HEADS = (MIX_WIDTH // 2) // DN_HEAD_DIM
DN_WIDTH = DN_HEADS * DN_HEAD_DIM
CONV_K = 4
DN_CHUNK = 64
HG_HEAD_DIM = 128
HG_HEADS = (MIX_WIDTH - DN_WIDTH) // HG_HEAD_DIM
HG_WIDTH = HG_HEADS * HG_HEAD_DIM
HG_CHUNK = 16
DN_Q = 0
DN_K = DN_Q + DN_WIDTH
DN_V = DN_K + DN_WIDTH
DN_Z = DN_V + DN_WIDTH
DN_BETA = DN_Z + DN_WIDTH
DN_ALPHA = DN_BETA + DN_HEADS
HG_Q = DN_ALPHA + DN_HEADS
HG_F = HG_Q + HG_WIDTH
HG_I = HG_F + HG_WIDTH
HG_Z = HG_I + HG_WIDTH
EVEN_IN_COLS = HG_Z + HG_WIDTH
RW_HEAD_DIM = 64
RW_HEADS = D_MODEL // RW_HEAD_DIM
DECAY_LORA = max(32, int(round(1.8 * D_MODEL ** 0.5 / 32)) * 32)
A_LORA = max(32, int(round(1.8 * D_MODEL ** 0.5 / 32)) * 32)
V_LORA = max(32, int(round(1.3 * D_MODEL ** 0.5 / 32)) * 32)
NORM_EPS = 1e-6
GN_EPS = 64e-5
L2_EPS = 1e-6

kernel_name = "hybrid_deltanet_hgrn2_rwkv7_trunk"


def rms_norm(x, gain):
    xf = x.astype(jnp.float32)
    y = xf * lax.rsqrt(jnp.mean(xf * xf, axis=-1, keepdims=True) + NORM_EPS)
    return (y * gain.astype(jnp.float32)).astype(x.dtype)


def l2norm(x):
    return x * lax.rsqrt(jnp.sum(x * x, axis=-1, keepdims=True) + L2_EPS)


def causal_depthwise_conv(x, w):
    k_len, t_len = w.shape[0], x.shape[1]
    xp = jnp.pad(x, ((0, 0), (k_len - 1, 0), (0, 0)))
    return sum(xp[:, j:j + t_len] * w[j] for j in range(k_len))


def to_chunks(x, c):
    b, t, h, d = x.shape
    return x.reshape(b, t // c, c, h, d).transpose(0, 3, 1, 2, 4)


def gated_delta_rule_chunked(q, k, v, beta, log_alpha):
    b, t, h, dk = q.shape
    dv = v.shape[-1]
    c = DN_CHUNK
    n = t // c
    qc, kc, vc = to_chunks(q, c), to_chunks(k, c), to_chunks(v, c)
    bc = beta.reshape(b, n, c, h).transpose(0, 3, 1, 2)
    G = jnp.cumsum(log_alpha.reshape(b, n, c, h).transpose(0, 3, 1, 2), axis=-1)
    causal = jnp.tril(jnp.ones((c, c), bool))
    strict = jnp.tril(jnp.ones((c, c), bool), -1)
    decay = jnp.exp(jnp.where(causal, G[..., :, None] - G[..., None, :], -jnp.inf))
    kk = jnp.einsum('bhntd,bhnsd->bhnts', kc, kc)
    lower = jnp.where(strict, bc[..., :, None] * kk * decay, 0.0)
    gamma = jnp.exp(G)
    rhs = jnp.concatenate([vc * bc[..., None], kc * (bc * gamma)[..., None]], axis=-1)
    eye = jnp.eye(c, dtype=jnp.float32)
    sol = lax.linalg.triangular_solve(eye + lower, rhs, left_side=True, lower=True,
                                      unit_diagonal=True)
    u0, w = sol[..., :dv], sol[..., dv:]
    a_qk = jnp.einsum('bhntd,bhnsd->bhnts', qc, kc) * decay
    q_dec = qc * gamma[..., None]
    g_last = G[..., -1]
    k_dec = kc * jnp.exp(g_last[..., None] - G)[..., None]
    chunk_decay = jnp.exp(g_last)

    def step(S, inp):
        u0_n, w_n, aqk_n, qd_n, kd_n, cd_n = inp
        u = u0_n - jnp.einsum('bhck,bhkv->bhcv', w_n, S)
        o = jnp.einsum('bhck,bhkv->bhcv', qd_n, S) + jnp.einsum('bhts,bhsv->bhtv', aqk_n, u)
        S = cd_n[..., None, None] * S + jnp.einsum('bhck,bhcv->bhkv', kd_n, u)
        return S, o

    xs = tuple(jnp.moveaxis(a, 2, 0) for a in (u0, w, a_qk, q_dec, k_dec, chunk_decay))
    _, o = lax.scan(step, jnp.zeros((b, h, dk, dv), jnp.float32), xs)
    return o.transpose(1, 0, 3, 2, 4).reshape(b, t, h, dv)


def gla_chunked(q, k, v, log_f):
    b, t, h, dk = q.shape
    dv = v.shape[-1]
    c = HG_CHUNK
    causal = jnp.tril(jnp.ones((c, c), bool))[:, :, None]
    qc, kc, vc = to_chunks(q, c), to_chunks(k, c), to_chunks(v, c)
    bcum = jnp.cumsum(to_chunks(log_f, c), axis=3)

    def step(S, inp):
        q_n, k_n, v_n, b_n = inp
        diff = b_n[:, :, :, None, :] - b_n[:, :, None, :, :]
        dec = jnp.exp(jnp.where(causal, diff, -jnp.inf))
        att = jnp.einsum('bhtd,bhsd,bhtsd->bhts', q_n, k_n, dec)
        b_last = b_n[:, :, -1]
        o = (jnp.einsum('bhtk,bhkv->bhtv', q_n * jnp.exp(b_n), S)
             + jnp.einsum('bhts,bhsv->bhtv', att, v_n))
        S = (jnp.exp(b_last)[..., None] * S
             + jnp.einsum('bhsk,bhsv->bhkv', k_n * jnp.exp(b_last[:, :, None, :] - b_n), v_n))
        return S, o

    xs = tuple(jnp.moveaxis(a, 2, 0) for a in (qc, kc, vc, bcum))
    _, o = lax.scan(step, jnp.zeros((b, h, dk, dv), jnp.float32), xs)
    return o.transpose(1, 0, 3, 2, 4).reshape(b, t, h, dv)


def rwkv7_scan(r, w, k, v, a, bb):
    b, t, h, d = r.shape

    def step(S, inp):
        r_t, w_t, k_t, v_t, a_t, b_t = inp
        sa = jnp.einsum('bhvk,bhk->bhv', S, a_t)
        S = S * w_t[:, :, None, :] + sa[..., None] * b_t[:, :, None, :] + v_t[..., None] * k_t[:, :, None, :]
        return S, jnp.einsum('bhvk,bhk->bhv', S, r_t)

    xs = tuple(jnp.moveaxis(z, 1, 0) for z in (r, w, k, v, a, bb))
    _, y = lax.scan(step, jnp.zeros((b, h, d, d), jnp.float32), xs)
    return jnp.moveaxis(y, 0, 1)


def head_group_norm(y, w, b):
    mu = jnp.mean(y, axis=-1, keepdims=True)
    var = jnp.mean(jnp.square(y - mu), axis=-1, keepdims=True)
    hs = y.shape[-2:]
    return (y - mu) * lax.rsqrt(var + GN_EPS) * w.reshape(hs) + b.reshape(hs)


def even_layer(hn, w_in, conv_w, a_log, dt_bias, dn_norm, lower_bound, hg_norm, w_out):
    b, t, _ = hn.shape
    f32 = jnp.float32
    p = hn @ w_in
    qkv = jax.nn.silu(causal_depthwise_conv(p[..., DN_Q:DN_Z], conv_w)).astype(f32)
    heads = lambda z, nh, hd: z.reshape(b, t, nh, hd)
    dq = l2norm(heads(qkv[..., :DN_WIDTH], DN_HEADS, DN_HEAD_DIM)) * DN_HEAD_DIM ** -0.5
    dk = l2norm(heads(qkv[..., DN_WIDTH:2 * DN_WIDTH], DN_HEADS, DN_HEAD_DIM))
    dv = heads(qkv[..., 2 * DN_WIDTH:], DN_HEADS, DN_HEAD_DIM)
    beta = jax.nn.sigmoid(p[..., DN_BETA:DN_ALPHA].astype(f32))
    log_alpha = -jnp.exp(a_log.astype(f32)) * jax.nn.softplus(
        p[..., DN_ALPHA:HG_Q].astype(f32) + dt_bias.astype(f32))
    o_a = gated_delta_rule_chunked(dq, dk, dv, beta, log_alpha)
    z_a = heads(p[..., DN_Z:DN_BETA].astype(f32), DN_HEADS, DN_HEAD_DIM)
    o_a = rms_norm(o_a, dn_norm) * jax.nn.silu(z_a)
    hq = heads(jax.nn.silu(p[..., HG_Q:HG_F].astype(f32)), HG_HEADS, HG_HEAD_DIM) * HG_HEAD_DIM ** -0.5
    forget = lower_bound + (1.0 - lower_bound) * jax.nn.sigmoid(p[..., HG_F:HG_I].astype(f32))
    hk = heads(1.0 - forget, HG_HEADS, HG_HEAD_DIM)
    log_f = heads(jnp.log(forget), HG_HEADS, HG_HEAD_DIM)
    hv = heads(p[..., HG_I:HG_Z].astype(f32), HG_HEADS, HG_HEAD_DIM)
    o_b = gla_chunked(hq, hk, hv, log_f)
    z_b = heads(p[..., HG_Z:].astype(f32), HG_HEADS, HG_HEAD_DIM)
    o_b = rms_norm(o_b, hg_norm) * jax.nn.silu(z_b)
    o = jnp.concatenate([o_a.reshape(b, t, DN_WIDTH), o_b.reshape(b, t, HG_WIDTH)], axis=-1)
    return o.astype(hn.dtype) @ w_out


def odd_layer(hn, v_first, mu, w_rkvz, w0, w1, w2, a0, a1, a2, v_res, k_k, k_a, r_k,
              ln_w, ln_b, w_out):
    b, t, d = hn.shape
    f32 = jnp.float32
    xx = jnp.pad(hn, ((0, 0), (1, 0), (0, 0)))[:, :-1] - hn
    mix = lambda i: hn + xx * mu[i]
    xv = mix(3)
    r = (mix(0) @ w_rkvz[0]).astype(f32)
    k = (mix(2) @ w_rkvz[1]).astype(f32)
    v = (xv @ w_rkvz[2]).astype(f32)
    z = (mix(5) @ w_rkvz[3]).astype(f32)
    log_w = -jax.nn.softplus(-(w0 + jnp.tanh(mix(1) @ w1) @ w2).astype(f32)) - 0.5
    decay = jnp.exp(-jnp.exp(log_w))
    if v_first is None:
        v_first = v
    else:
        v0, v1, v2 = v_res
        v = v + (v_first - v) * jax.nn.sigmoid((v0 + (xv @ v1) @ v2).astype(f32))
    a = jax.nn.sigmoid((a0 + (mix(4) @ a1) @ a2).astype(f32))
    heads = lambda u: u.reshape(b, t, RW_HEADS, RW_HEAD_DIM)
    kk = l2norm(heads(k * k_k.astype(f32)))
    k = k * (1.0 + (a - 1.0) * k_a.astype(f32))
    rh, kh, vh = heads(r), heads(k), heads(v)
    y = rwkv7_scan(rh, heads(decay), kh, vh, -kk, kk * heads(a))
    y = head_group_norm(y, ln_w.astype(f32), ln_b.astype(f32))
    y = y + jnp.sum(rh * kh * r_k.astype(f32), axis=-1, keepdims=True) * vh
    y = y.reshape(b, t, d) * jax.nn.silu(z)
    return y.astype(hn.dtype) @ w_out, v_first


def setup_inputs(seed: int = 0) -> dict:
    key = jax.random.key(seed)
    ks = iter(jax.random.split(key, 48))
    nrm = lambda shape, s: jax.random.normal(next(ks), shape, jnp.float32) * s
    uni = lambda shape, lo, hi: jax.random.uniform(next(ks), shape, jnp.float32, lo, hi)
    D = D_MODEL
    dt = jnp.exp(uni((N_EVEN, DN_HEADS), math.log(1e-3), math.log(1e-1)))
    return {
        "x": nrm((BATCH, SEQ, D), 1.0),
        "norm_gains": 1.0 + nrm((DEPTH, D), 0.02),
        "mix_w_in": nrm((N_EVEN, D, EVEN_IN_COLS), D ** -0.5),
        "dn_conv": nrm((N_EVEN, CONV_K, 3 * DN_WIDTH), CONV_K ** -0.5),
        "dn_a_log": jnp.log(uni((N_EVEN, DN_HEADS), 1.0, 16.0)),
        "dn_dt_bias": dt + jnp.log(-jnp.expm1(-dt)),
        "dn_out_norm": 1.0 + nrm((N_EVEN, DN_HEAD_DIM), 0.02),
        "hg_lb_logits": nrm((N_EVEN, HG_WIDTH), 0.5),
        "hg_out_norm": 1.0 + nrm((N_EVEN, HG_HEAD_DIM), 0.02),
        "mix_w_out": nrm((N_EVEN, MIX_WIDTH, D), MIX_WIDTH ** -0.5),
        "rw_mu": uni((N_ODD, 6, D), 0.0, 1.0),
        "rw_w_rkvz": nrm((N_ODD, 4, D, D), D ** -0.5),
        "rw_w0": uni((N_ODD, D), -6.0, -1.0),
        "rw_w1": nrm((N_ODD, D, DECAY_LORA), D ** -0.5),
        "rw_w2": nrm((N_ODD, DECAY_LORA, D), 0.1 * DECAY_LORA ** -0.5),
        "rw_a0": nrm((N_ODD, D), 0.1),
        "rw_a1": nrm((N_ODD, D, A_LORA), D ** -0.5),
        "rw_a2": nrm((N_ODD, A_LORA, D), 0.1 * A_LORA ** -0.5),
        "rw_v0": 1.0 + nrm((N_V_RES, D), 0.1),
        "rw_v1": nrm((N_V_RES, D, V_LORA), D ** -0.5),
        "rw_v2": nrm((N_V_RES, V_LORA, D), 0.1 * V_LORA ** -0.5),
        "rw_k_k": 0.85 + nrm((N_ODD, D), 0.02),
        "rw_k_a": 1.0 + nrm((N_ODD, D), 0.02),
        "rw_r_k": nrm((N_ODD, RW_HEADS, RW_HEAD_DIM), 0.1),
        "rw_ln_w": 1.0 + nrm((N_ODD, D), 0.02),
        "rw_ln_b": nrm((N_ODD, D), 0.02),
        "rw_w_out": nrm((N_ODD, D, D), D ** -0.5),
        "final_norm": 1.0 + nrm((D,), 0.02),
    }


def reference(x, norm_gains, mix_w_in, dn_conv, dn_a_log, dn_dt_bias, dn_out_norm,
              hg_lb_logits, hg_out_norm, mix_w_out, rw_mu, rw_w_rkvz, rw_w0, rw_w1, rw_w2,
              rw_a0, rw_a1, rw_a2, rw_v0, rw_v1, rw_v2, rw_k_k, rw_k_a, rw_r_k,
              rw_ln_w, rw_ln_b, rw_w_out, final_norm):
    lb_p = jax.nn.softmax(hg_lb_logits.astype(jnp.float32), axis=0)
    lower_bounds = jnp.cumsum(lb_p, axis=0) - lb_p[0]
    h = x
    v_first = None
    for layer in range(DEPTH):
        hn = rms_norm(h, norm_gains[layer])
        if layer % 2 == 0:
            e = layer // 2
            h = h + even_layer(hn, mix_w_in[e], dn_conv[e], dn_a_log[e], dn_dt_bias[e],
                               dn_out_norm[e], lower_bounds[e], hg_out_norm[e], mix_w_out[e])
        else:
            o = layer // 2
            v_res = None if o == 0 else (rw_v0[o - 1], rw_v1[o - 1], rw_v2[o - 1])
            out, v_first = odd_layer(hn, v_first, rw_mu[o], rw_w_rkvz[o], rw_w0[o], rw_w1[o],
                                     rw_w2[o], rw_a0[o], rw_a1[o], rw_a2[o], v_res, rw_k_k[o],
                                     rw_k_a[o], rw_r_k[o], rw_ln_w[o], rw_ln_b[o], rw_w_out[o])
            h = h + out
    return rms_norm(h, final_norm)
```

```python
import numpy as np
import ml_dtypes
from contextlib import ExitStack
import concourse.bass as bass
import concourse.mybir as mybir
from concourse.bass_utils import run_bass_kernel_spmd

F32 = mybir.dt.float32
BF16 = mybir.dt.bfloat16
AF = mybir.ActivationFunctionType
ALU = mybir.AluOpType

D = 2048
SEQ = 4096
NB = 8
TT = 256
NTB = TT // 128
EPOCH = 30000
NDMA_SEM = 6
EIN = 8208
import os as _os
MAXOPS = int(_os.environ.get("DEV_MAXOPS", "1000000000"))


class Prog:
    ENGS = ("pe", "act", "dve", "pool", "sp")

    def __init__(self, nc, same_engine_sync=True):
        self.nc = nc
        self.same = same_engine_sync
        self.ops = {e: [] for e in self.ENGS}
        self.cnt = {e: 0 for e in self.ENGS}
        self.known = {e: {} for e in self.ENGS}
        self.lastw = {}
        self.readers = {}
        self.dma_n = {e: 0 for e in self.ENGS}
        self.dma_slot_cnt = {}
        self.semkeys = set()

    def _need(self, eng, tok, waits):
        if tok is None:
            return
        semkey, val, src = tok
        if src == eng and (eng == "pe" or not self.same):
            return
        if self.known[eng].get(semkey, 0) >= val:
            return
        self.known[eng][semkey] = val
        waits.append((semkey, val))

    def _deps(self, eng, reads, writes):
        waits = []
        for r in reads:
            self._need(eng, self.lastw.get(r), waits)
        for w in writes:
            self._need(eng, self.lastw.get(w), waits)
            for t in self.readers.get(w, ()):
                self._need(eng, t, waits)
        return waits

    def _commit(self, tok, reads, writes):
        for r in reads:
            self.readers.setdefault(r, []).append(tok)
        for w in writes:
            self.lastw[w] = tok
            self.readers[w] = []

    def op(self, eng, fn, reads=(), writes=()):
        self.total = getattr(self, "total", 0) + 1
        if self.total > MAXOPS:
            return
        writes = list(writes) + [r for r in reads if r.startswith("ps") and r not in writes]
        waits = self._deps(eng, reads, writes)
        i = self.cnt[eng]
        self.cnt[eng] = i + 1
        semkey = (eng, i // EPOCH)
        self.semkeys.add(semkey)
        tok = (semkey, (i % EPOCH) + 1, eng)
        self.ops[eng].append((waits, fn, semkey, 1))
        self._commit(tok, reads, writes)

    def dma(self, q, out, in_, reads=(), writes=(), **kw):
        self.total = getattr(self, "total", 0) + 1
        if self.total > MAXOPS:
            return
        waits = self._deps(q, reads, writes)
        n = self.dma_n[q]
        self.dma_n[q] = n + 1
        slot = n % NDMA_SEM
        semkey = ("dma_" + q, slot)
        self.semkeys.add(semkey)
        uses = self.dma_slot_cnt.get((q, slot), 0)
        if uses > 0:
            self._need(q, (semkey, 16 * uses, None), waits)
        self.dma_slot_cnt[(q, slot)] = uses + 1
        tok = (semkey, 16 * (uses + 1), None)
        self.ops[q].append((waits, (lambda e, o=out, i=in_, k=kw: e.dma_start(out=o, in_=i, **k)), semkey, 16))
        self._commit(tok, reads, writes)

    def emit(self, final_eng="sp"):
        nc = self.nc
        closing = []
        for (q, slot), uses in self.dma_slot_cnt.items():
            self._need(final_eng, (("dma_" + q, slot), 16 * uses, None), closing)
        for e2 in self.ENGS:
            n = self.cnt[e2]
            if n > 0:
                self._need(final_eng, ((e2, (n - 1) // EPOCH), ((n - 1) % EPOCH) + 1, None), closing)
        with ExitStack() as st:
            sems = {}
            for k in sorted(self.semkeys, key=str):
                sems[k] = st.enter_context(nc.semaphore("s_%s_%s" % k))
            block = st.enter_context(nc.Block())
            hmap = {"pe": "tensor", "act": "scalar", "dve": "vector", "pool": "gpsimd", "sp": "sync"}

            def mk(eng):
                oplist = self.ops[eng]

                def body(e):
                    for waits, fn, semkey, inc in oplist:
                        for (sk, v) in waits:
                            e.wait_ge(sems[sk], v)
                        fn(e).then_inc(sems[semkey], inc)
                    if eng == final_eng:
                        for (sk, v) in closing:
                            e.wait_ge(sems[sk], v)
                return body

            for eng in self.ENGS:
                if self.ops[eng] or eng == final_eng:
                    getattr(block, hmap[eng])(mk(eng))


def host_consts():
    c = {}
    idx = np.arange(128)
    s = idx[:, None]
    t = idx[None, :]
    same64 = (s // 64) == (t // 64)
    same32 = (s // 32) == (t // 32)
    c["identb"] = np.eye(128).astype(ml_dtypes.bfloat16)
    f = {}
    f["identf"] = np.eye(128)
    f["ones128"] = np.ones((128, 128))
    f["ones64bd"] = same64 * 1.0
    f["m_st_incl64"] = (same64 & (s <= t)) * 1.0
    f["m_st_strict64"] = (same64 & (s < t)) * 1.0
    f["m_st_sneg64"] = (same64 & (s < t)) * -1.0
    f["m_ts_strict64"] = (same64 & (s > t)) * 1.0
    f["m_ts_sneg64"] = (same64 & (s > t)) * -1.0
    f["m_st_incl32"] = (same32 & (s <= t)) * 1.0
    names = list(f.keys())
    c["cf"] = np.concatenate([f[k] for k in names], axis=1).astype(np.float32)
    c["_cf_names"] = names
    rm = np.zeros((128, 4), np.float32)
    rm[idx, idx // 32] = 1.0
    tt = np.arange(TT)
    r64 = np.broadcast_to(((tt % 64) != 0) * 1.0, (128, TT))
    r32 = np.broadcast_to(((tt % 32) != 0) * 1.0, (128, TT))
    c["cm"] = np.concatenate([rm, r64, r32], axis=1).astype(np.float32)
    return c


class Builder:
    def __init__(self, T, nlayers, final_norm=True):
        self.T = T
        self.NT = T // TT
        self.nlayers = nlayers
        self.final = final_norm
        self.nc = bass.Bass("TRN2", target_bir_lowering=False)
        self.P = Prog(self.nc)
        self.st = ExitStack()
        self.bank_i = {}

    def sb(self, name, shape, dt=F32):
        return self.st.enter_context(self.nc.sbuf_tensor("sb_" + name, shape, dt))

    def dram_in(self, name, shape, dt=F32):
        return self.nc.dram_tensor(name, list(shape), dt, kind="ExternalInput").ap()

    def dram_tmp(self, name, shape, dt=F32):
        return self.nc.dram_tensor(name, list(shape), dt).ap()

    def MM(self, out, lhsT, rhs, r, w, start=True, stop=True):
        self.P.op("pe", lambda e: e.matmul(out=out, lhsT=lhsT, rhs=rhs, start=start, stop=stop), r, w)

    def TR(self, out, in_, ident, r, w):
        self.P.op("pe", lambda e: e.transpose(out=out, in_=in_, identity=ident), r, w)

    def ACT(self, out, in_, func, r, w, scale=1.0, bias=None, accum=None):
        kw = {}
        if bias is not None:
            kw["bias"] = bias
        if accum is not None:
            kw["accum_out"] = accum
        self.P.op("act", lambda e: e.activation(out=out, in_=in_, func=func, scale=scale, **kw), r, w)

    def TTo(self, eng, out, a, b, op, r, w):
        self.P.op(eng, lambda e: e.tensor_tensor(out=out, in0=a, in1=b, op=op), r, w)

    def TS(self, eng, out, a, s1, s2, op0, op1, r, w):
        if op1 is None:
            self.P.op(eng, lambda e: e.tensor_scalar(out=out, in0=a, scalar1=s1, scalar2=None, op0=op0), r, w)
        else:
            self.P.op(eng, lambda e: e.tensor_scalar(out=out, in0=a, scalar1=s1, scalar2=s2, op0=op0, op1=op1), r, w)

    def STT(self, out, in0, scalar, in1, op0, op1, r, w):
        self.P.op("dve", lambda e: e.scalar_tensor_tensor(out=out, in0=in0, scalar=scalar, in1=in1, op0=op0, op1=op1), r, w)

    def CP(self, eng, out, in_, r, w):
        if eng == "act":
            self.P.op("act", lambda e: e.copy(out=out, in_=in_), r, w)
        else:
            self.P.op(eng, lambda e: e.tensor_copy(out=out, in_=in_), r, w)

    def SCAN(self, out, d0, d1, r, w):
        self.P.op("dve", lambda e: e.tensor_tensor_scan(out=out, data0=d0, data1=d1, initial=0.0,
                                                        op0=ALU.mult, op1=ALU.add), r, w)

    def RECIP(self, out, in_, r, w):
        self.P.op("dve", lambda e: e.reciprocal(out=out, in_=in_), r, w)

    def MEMSET(self, eng, ap, val, w):
        self.P.op(eng, lambda e: e.memset(ap, val), (), w)

    def DMA(self, q, out, in_, r, w, **kw):
        self.P.dma(q, out, in_, r, w, **kw)

    def bank(self, grp):
        lst = self.banks[grp]
        i = self.bank_i.get(grp, 0)
        self.bank_i[grp] = i + 1
        return lst[i % len(lst)]

    def RSQRT(self, out, in_, scale, eps, r, w):
        self.ACT(out, in_, AF.Sqrt, r, w, scale=scale, bias=eps)
        self.RECIP(out, out, w, w)

    def build(self):
        nc, T = self.nc, self.T
        sb = self.sb
        di = self.dram_in
        I = {}
        I["x"] = di("x", [T, D])
        I["norm_gains"] = di("norm_gains", [4, D])
        I["mix_w_in"] = di("mix_w_in", [2, D, EIN])
        I["dn_conv"] = di("dn_conv", [2, 4, 3072])
        I["dn_a_log"] = di("dn_a_log", [2, 8])
        I["dn_dt_bias"] = di("dn_dt_bias", [2, 8])
        I["dn_out_norm"] = di("dn_out_norm", [2, 128])
        I["hg_lb_logits"] = di("hg_lb_logits", [2, 1024])
        I["hg_out_norm"] = di("hg_out_norm", [2, 128])
        I["mix_w_out"] = di("mix_w_out", [2, D, D])
        I["rw_mu"] = di("rw_mu", [2, 6, D])
        I["rw_w_rkvz"] = di("rw_w_rkvz", [2, 4, D, D])
        I["rw_w0"] = di("rw_w0", [2, D])
        I["rw_w1"] = di("rw_w1", [2, D, 96])
        I["rw_w2"] = di("rw_w2", [2, 96, D])
        I["rw_a0"] = di("rw_a0", [2, D])
        I["rw_a1"] = di("rw_a1", [2, D, 96])
        I["rw_a2"] = di("rw_a2", [2, 96, D])
        I["rw_v0"] = di("rw_v0", [1, D])
        I["rw_v1"] = di("rw_v1", [1, D, 64])
        I["rw_v2"] = di("rw_v2", [1, 64, D])
        I["rw_k_k"] = di("rw_k_k", [2, D])
        I["rw_k_a"] = di("rw_k_a", [2, D])
        I["rw_r_k"] = di("rw_r_k", [2, D])
        I["rw_ln_w"] = di("rw_ln_w", [2, D])
        I["rw_ln_b"] = di("rw_ln_b", [2, D])
        I["rw_w_out"] = di("rw_w_out", [2, D, D])
        I["final_norm"] = di("final_norm", [1, D])
        I["identb"] = di("identb", [128, 128], BF16)
        I["cf"] = di("cf", [128, 9 * 128])
        I["cm"] = di("cm", [128, 4 + 2 * TT])
        self.I = I
        self.y = nc.dram_tensor("y", [T, D], F32, kind="ExternalOutput").ap()
        dt_ = self.dram_tmp
        self.hD = dt_("h_scr", [T, D])
        self.vfD = dt_("vf_scr", [D, T])
        self.win_bf = [dt_("win_bf%d" % e, [64, 128, 16, 128], BF16) for e in range(2)]
        self.wba_bf = [dt_("wba_bf%d" % e, [128, 16, 16], BF16) for e in range(2)]
        self.wout_bf = [dt_("wout_bf%d" % l, [4, 128, 16, 512], BF16) for l in range(4)]
        self.wr_bf = [dt_("wr_bf%d" % o, [4, 16, 128, 16, 128], BF16) for o in range(2)]

        self.cf = sb("cf", [128, 9 * 128])
        self.cm = sb("cm", [128, 4 + 2 * TT])
        self.identb = sb("identb", [128, 128], BF16)
        names = host_consts()["_cf_names"]
        self.C = {n: self.cf[:, i * 128:(i + 1) * 128] for i, n in enumerate(names)}
        self.rowmask32 = self.cm[:, 0:4]
        self.reset64 = self.cm[:, 4:4 + TT]
        self.reset32 = self.cm[:, 4 + TT:4 + 2 * TT]
        self.hres = sb("hres", [128, NTB, D])
        self.A8 = sb("A8", [128, 2, D], BF16)
        self.ogT = self.A8[:].rearrange("p a (b t) -> p (a b) t", t=TT)
        self.gain = sb("gain", [128, D])
        self.mixb = sb("mixb", [128, 6, 16 * (TT + 1)], BF16)
        self.wbuf = [sb("wbuf%d" % i, [128, 8192], BF16) for i in range(2)]
        self.tmp = sb("tmp", [128, 34 * TT])
        self.ga_t = sb("ga_t", [128, TT])
        self.gacol_t = sb("gacol_t", [128, 128])
        self.sq = sb("sqt", [128, 30 * 128])
        self.small = sb("small", [128, 16])
        self.par_t = sb("par_t", [128, 256])
        self.halo_t = sb("halo_t", [128, 72])
        self.stS = sb("stS", [128, 16 * 128])
        self.pp = sb("pp", [128, 256])
        self.lora = sb("lora", [128, 10240], BF16)
        self.hnTo = sb("hnTo", [128, 16, TT + 1], BF16)
        self.ps = [self.st.enter_context(nc.psum_tensor("ps%d" % i, [128, 512], F32)) for i in range(6)]
        self.psb = [self.st.enter_context(nc.psum_tensor("psb%d" % i, [128, 1024], BF16)) for i in range(2)]
        self.banks = {
            "proj": [(self.ps[0], "ps0"), (self.ps[1], "ps1")],
            "mix": [(self.ps[2], "ps2"), (self.ps[3], "ps3"), (self.ps[4], "ps4")],
            "acc": [(self.ps[5], "ps5")],
        }
        self.wb_i = 0
        self.tmp_i = 0
        self.sq_i = 0

        P = self.P
        self.DMA("sp", self.cf[:], I["cf"], [], ["cf"])
        self.DMA("sp", self.cm[:], I["cm"], [], ["cm"])
        self.DMA("sp", self.identb[:], I["identb"], [], ["identb"])

        self.prologue()
        self.MEMSET("dve", self.small[:, 8:9], 0.0, ["tmp_stage0", "tmp_stage1"] + ["tmp%d" % j for j in range(34)])
        for layer in range(self.nlayers):
            if layer % 2 == 0:
                self.even_layer(layer)
            else:
                self.odd_layer(layer)
        P.emit()
        self.st.close()
        return nc

    def T_(self, n=1):
        if self.tmp_i + n > 34:
            self.tmp_i = 0
        i = self.tmp_i
        self.tmp_i += n
        return self.tmp[:, i * TT:(i + n) * TT], ["tmp%d" % j for j in range(i, i + n)]

    def Q_(self):
        i = self.sq_i % 30
        self.sq_i += 1
        return self.sq[:, i * 128:(i + 1) * 128], ["sq%d" % i]

    def conv_cols(self, src, c0, n, dst_fn, dkey, tag):
        for k in range(16):
            par = self.cv_i % 2
            self.cv_i += 1
            sf = self.tmp[:, par * 2052:par * 2052 + n]
            sbf = self.mixb[:, par, 0:n]
            kf, kb = "tmp_stage%d" % par, "mix%d" % par
            self.DMA("sp", sf, src[k * 128:(k + 1) * 128, c0:c0 + n], [], [kf])
            eng = ("dve", "act", "pool")[self.cv_i % 3]
            self.CP(eng, sbf, sf, [kf], [kb])
            d = dst_fn(k)
            s_ = sbf if len(d.shape) == 2 else sbf.rearrange("p (a c) -> p a c", c=d.shape[2])
            self.DMA("pool", d, s_, [kb], [dkey])

    def prologue(self):
        I = self.I
        self.cv_i = 0
        nl = self.nlayers
        for e in range(2):
            if 2 * e >= nl:
                break
            w = I["mix_w_in"][e]
            wb = self.win_bf[e]
            for (c0, cb0) in ((0, 0), (2048, 16), (4112, 32), (6160, 48)):
                self.conv_cols(w, c0, 2048,
                               lambda k, cb0=cb0, wb=wb: wb[cb0:cb0 + 16, :, k, :].rearrange("cb p c -> p cb c"),
                               "win_bf%d" % e, "win")
            self.conv_cols(w, 4096, 16, lambda k, e=e: self.wba_bf[e][:, k, :], "wba_bf%d" % e, "wba")
            wo = self.wout_bf[2 * e]
            self.conv_cols(I["mix_w_out"][e], 0, 2048,
                           lambda k, wo=wo: wo[:, :, k, :].rearrange("cg p c -> p cg c"), "wout_bf%d" % (2 * e), "wout")
        for o in range(2):
            if 2 * o + 1 >= nl:
                break
            for i in range(4):
                wr = self.wr_bf[o]
                self.conv_cols(I["rw_w_rkvz"][o, i], 0, 2048,
                               lambda k, wr=wr, i=i: wr[i, :, :, k, :].rearrange("cb p c -> p cb c"),
                               "wr_bf%d" % o, "wr")
            wo = self.wout_bf[2 * o + 1]
            self.conv_cols(I["rw_w_out"][o], 0, 2048,
                           lambda k, wo=wo: wo[:, :, k, :].rearrange("cg p c -> p cg c"), "wout_bf%d" % (2 * o + 1), "wout")

    def load_pp(self, rows):
        stg, kst = self.Q_()
        r0 = 0
        for ap in rows:
            n = ap.shape[0]
            self.DMA("sp", stg[r0:r0 + n, :], ap, [], kst)
            r0 += n
        R = r0
        pb, pk = self.bank("mix")
        self.TR(pb[:, 0:R], stg[0:R, :], self.C["identf"][0:R, 0:R], kst + ["cf"], [pk])
        self.CP("dve", self.pp[:, 0:R], pb[:, 0:R], [pk], ["pp"])
        return R

    def load_norm_T(self, layer, tt, dstT, dkeys, halo):
        src = self.I["x"] if layer == 0 else self.hD
        for tb in range(NTB):
            r0 = tt * TT + tb * 128
            hk = "hres%d" % tb
            self.DMA("sp", self.hres[:, tb, :], src[r0:r0 + 128, :], ["hrow%d_%d" % (tt, tb)], [hk])
            junk = self.mixb[:, 5, 0:D]
            ss = self.small[:, 0:1]
            rstd = self.small[:, 1:2]
            self.ACT(junk, self.hres[:, tb, :], AF.Square, [hk], ["mix5", "sm_ss"], accum=ss)
            self.RSQRT(rstd, ss, 1.0 / D, 1e-6, ["sm_ss"], ["sm_rstd"])
            hn = self.A8[:, tb % 2, :]
            hnk = "A8_%d" % (tb % 2)
            self.STT(hn, self.hres[:, tb, :], rstd, self.gain[:], ALU.mult, ALU.mult, [hk, "sm_rstd", "gain"], [hnk])
            for half in range(2):
                pb, pk = self.psb[half], "psb%d" % half
                for k in range(8):
                    kk = half * 8 + k
                    self.TR(pb[:, k * 128:(k + 1) * 128], hn[:, kk * 128:(kk + 1) * 128], self.identb[:],
                            [hnk, "identb"], [pk])
                eng = "dve" if half == 0 else "act"
                self.CP(eng, dstT[:, half * 8:(half + 1) * 8, halo + tb * 128:halo + (tb + 1) * 128],
                        pb[:].rearrange("p (k t) -> p k t", t=128), [pk], dkeys)

    def out_proj(self, layer, tt):
        last = (layer == self.nlayers - 1)
        wo = self.wout_bf[layer]
        for cg in range(4):
            wbt = self.wbuf[self.wb_i % 2]
            wk = "wbuf%d" % (self.wb_i % 2)
            self.wb_i += 1
            self.DMA("sp", wbt[:, :], wo[cg].rearrange("p k c -> p (k c)"), ["wout_bf%d" % layer], [wk])
            wv = wbt[:, :].rearrange("p (k c) -> p k c", c=512)
            for tb in range(NTB):
                pb, pk = self.bank("proj")
                for k in range(16):
                    self.MM(pb[:, 0:512], self.ogT[:, k, tb * 128:(tb + 1) * 128], wv[:, k, :],
                            ["A8_0", "A8_1", wk], [pk], start=(k == 0), stop=(k == 15))
                hk = "hres%d" % tb
                self.TTo("dve", self.hres[:, tb, cg * 512:(cg + 1) * 512], self.hres[:, tb, cg * 512:(cg + 1) * 512],
                         pb[:, 0:512], ALU.add, [pk, hk], [hk])
        for tb in range(NTB):
            r0 = tt * TT + tb * 128
            hk = "hres%d" % tb
            if last and self.final:
                fg = self.A8[:].rearrange("p a d -> p (a d)").bitcast(F32)
                if tb == 0:
                    self.DMA("sp", fg, self.I["final_norm"][0:1, :].partition_broadcast(128), [], ["A8_0", "A8_1"])
                junk = self.mixb[:, 5, 0:D]
                ss = self.small[:, 0:1]
                rstd = self.small[:, 1:2]
                self.ACT(junk, self.hres[:, tb, :], AF.Square, [hk], ["mix5", "sm_ss"], accum=ss)
                self.RSQRT(rstd, ss, 1.0 / D, 1e-6, ["sm_ss"], ["sm_rstd"])
                self.STT(self.hres[:, tb, :], self.hres[:, tb, :], rstd, fg, ALU.mult, ALU.mult,
                         [hk, "sm_rstd", "A8_0", "A8_1"], [hk])
                self.DMA("sp", self.y[r0:r0 + 128, :], self.hres[:, tb, :], [hk], ["yrow%d_%d" % (tt, tb)])
            elif last:
                self.DMA("sp", self.y[r0:r0 + 128, :], self.hres[:, tb, :], [hk], ["yrow%d_%d" % (tt, tb)])
            else:
                self.DMA("sp", self.hD[r0:r0 + 128, :], self.hres[:, tb, :], [hk], ["hrow%d_%d" % (tt, tb)])

    def load_gain(self, ap_row):
        self.DMA("sp", self.gain[:], ap_row.partition_broadcast(128), [], ["gain"])

    def proj_block(self, wv, wk, j, srcT, skeys, pb, pk, off=0):
        for k in range(16):
            self.MM(pb[:, off:off + TT], wv[:, j, k, :], srcT[:, k, :], [wk] + skeys, [pk], start=(k == 0), stop=(k == 15))

    def tri_inverse(self, X, Xk, XT, XTk):
        C = self.C
        Tt, Tk = self.Q_()
        self.TTo("pool", Tt, XT, C["identf"], ALU.add, XTk + ["cf"], Tk)
        for lvl in range(1, 6):
            lastl = (lvl == 5)
            pb, pk = self.bank("mix")
            self.MM(pb[:, 0:128], XT, X, Xk + XTk, [pk])
            if not lastl:
                self.MM(pb[:, 128:256], X, XT, Xk + XTk, [pk])
            Xn, Xnk = self.Q_()
            self.CP("act", Xn, pb[:, 0:128], [pk], Xnk)
            if not lastl:
                XTn, XTnk = self.Q_()
                self.CP("dve", XTn, pb[:, 128:256], [pk], XTnk)
            pb2, pk2 = self.bank("mix")
            self.MM(pb2[:, 0:128], Xn, Tt, Xnk + Tk, [pk2])
            Tn, Tnk = self.Q_()
            self.TTo("dve", Tn, Tt, pb2[:, 0:128], ALU.add, Tk + [pk2], Tnk)
            Tt, Tk = Tn, Tnk
            X, Xk = Xn, Xnk
            if not lastl:
                XT, XTk = XTn, XTnk
        return Tt, Tk

    def even_layer(self, layer):
        e = layer // 2
        I, C = self.I, self.C
        self.load_gain(I["norm_gains"][layer:layer + 1, :])
        R = self.load_pp([I["dn_conv"][e].rearrange("j (b p) -> (j b) p", p=128),
                          I["dn_out_norm"][e:e + 1, :], I["hg_out_norm"][e:e + 1, :],
                          I["hg_lb_logits"].rearrange("e (b p) -> (e b) p", p=128)])
        par = self.par_t
        self.CP("dve", par[:, 0:R], self.pp[:, 0:R], ["pp"], ["par"])
        convw = lambda j, b: par[:, j * 24 + b:j * 24 + b + 1]
        dn_g = par[:, 96:97]
        hg_g = par[:, 97:98]
        lb = par[:, 120:128]
        oml = par[:, 128:136]
        if e == 0:
            self.MEMSET("dve", lb, 0.0, ["par"])
            self.MEMSET("dve", oml, 1.0, ["par"])
        else:
            self.TTo("dve", lb, par[:, 106:114], par[:, 98:106], ALU.subtract, ["par"], ["par"])
            self.ACT(lb, lb, AF.Sigmoid, ["par"], ["par"])
            self.TS("dve", oml, lb, -1.0, 1.0, ALU.mult, ALU.add, ["par"], ["par"])
        ba = par[0:16, 136:144]
        mb, nA, dtb, mg = ba[:, 0:1], ba[:, 1:2], ba[:, 2:3], ba[:, 3:4]
        self.MEMSET("dve", par[0:16, 136:144], 0.0, ["par"])
        self.MEMSET("dve", par[0:8, 136:137], 1.0, ["par"])
        self.TS("dve", mg, mb, -1.0, 1.0, ALU.mult, ALU.add, ["par"], ["par"])
        self.DMA("sp", par[8:16, 137:138], I["dn_a_log"][e].rearrange("(h o) -> h o", o=1), ["par"], ["par"])
        self.DMA("sp", par[8:16, 138:139], I["dn_dt_bias"][e].rearrange("(h o) -> h o", o=1), ["par"], ["par"])
        self.ACT(nA, nA, AF.Exp, ["par"], ["par"])
        self.STT(nA, nA, -1.0, mg, ALU.mult, ALU.mult, ["par"], ["par"])
        wba = self.lora[:, 0:256].rearrange("p (k c) -> p k c", c=16)
        self.DMA("sp", self.lora[:, 0:256], self.wba_bf[e].rearrange("p k c -> p (k c)"), ["wba_bf%d" % e], ["lora"])
        Sg = self.stS[:, 0:1024].rearrange("p (h v) -> p h v", v=128)
        Sh = self.stS[:, 1024:2048].rearrange("p (h v) -> p h v", v=128)
        self.MEMSET("pool", self.stS[:, :], 0.0, ["Sg%d" % h for h in range(8)] + ["Sh%d" % h for h in range(8)])
        halo = self.halo_t
        self.MEMSET("pool", halo[:, :], 0.0, ["halo%d" % h for h in range(8)])
        hnT = self.mixb[:, 0, 0:16 * TT].rearrange("p (k t) -> p k t", t=TT)

        for tt in range(self.NT):
            self.load_norm_T(layer, tt, hnT, ["mix0"], 0)
            pb, pk = self.bank("proj")
            for k in range(16):
                self.MM(pb[0:16, 0:TT], wba[:, k, :], hnT[:, k, :], ["lora", "mix0"], [pk], start=(k == 0), stop=(k == 15))
            sig, sigk = self.T_()
            sp_, spk = self.T_()
            ga, gak = self.ga_t[:, :], ["ga_t"]
            self.ACT(sig[0:16, :], pb[0:16, 0:TT], AF.Sigmoid, [pk], sigk)
            self.ACT(sp_[0:16, :], pb[0:16, 0:TT], AF.Exp, [pk, "par"], spk, bias=dtb)
            self.ACT(sp_[0:16, :], sp_[0:16, :], AF.Ln, spk, spk, bias=1.0)
            self.TS("dve", sp_[0:16, :], sp_[0:16, :], nA, None, ALU.mult, None, spk + ["par"], spk)
            self.SCAN(ga[0:16, :], self.reset64[0:16, :], sp_[0:16, :], spk + ["cm"], gak)
            self.STT(ga[0:16, :], sig[0:16, :], mb, ga[0:16, :], ALU.mult, ALU.add, sigk + gak + ["par"], gak)
            gacol, gck = self.gacol_t[:, :], ["gacol_t"]
            for tb in range(NTB):
                pq, pqk = self.bank("mix")
                self.TR(pq[:, 0:16], ga[0:16, tb * 128:(tb + 1) * 128], C["identf"][0:16, 0:16], gak + ["cf"], [pqk])
                self.CP("dve", gacol[:, tb * 32:tb * 32 + 16], pq[:, 0:16], [pqk], gck)
                self.TS("dve", gacol[:, tb * 32 + 16:tb * 32 + 32], pq[:, 0:16], -1.0, None, ALU.mult, None, [pqk], gck)
            for h in range(8):
                self.gdn_head(e, h, tt, hnT, ga, gak, gacol, gck, convw, dn_g, halo, Sg)
            for h in range(8):
                self.hg_head(e, h, tt, hnT, lb, oml, hg_g, Sh)
            self.out_proj(layer, tt)

    def load_wblocks(self, src_aps, skey):
        wbt = self.wbuf[self.wb_i % 2]
        wk = "wbuf%d" % (self.wb_i % 2)
        self.wb_i += 1
        for j, ap in enumerate(src_aps):
            self.DMA("sp", wbt[:, j * 2048:(j + 1) * 2048], ap.rearrange("p k c -> p (k c)"), [skey], [wk])
        return wbt[:, :].rearrange("p (j k c) -> p j k c", k=16, c=128), wk

    def bcast_row(self, ga, gak, row, out, outk):
        C = self.C
        sel, selk = self.T_()
        self.TS("pool", sel[0:16, :], ga[0:16, :], C["identf"][0:16, row:row + 1], None, ALU.mult, None, gak + ["cf"], selk)
        pb, pk = self.bank("mix")
        self.MM(pb[:, 0:TT], C["ones128"][0:16, :], sel[0:16, :], selk + ["cf"], [pk])
        self.CP("act", out, pb[:, 0:TT], [pk], outk)

    def gdn_head(self, e, h, tt, hnT, ga, gak, gacol, gck, convw, dn_g, halo, Sg):
        C = self.C
        wb = self.win_bf[e]
        wv, wk = self.load_wblocks([wb[h], wb[8 + h], wb[16 + h], wb[24 + h]], "win_bf%d" % e)
        S = Sg[:, h, :]
        Sk = ["Sg%d" % h]
        act = []
        for j in range(3):
            pb, pk = self.bank("proj")
            self.proj_block(wv, wk, j, hnT, ["mix0"], pb, pk)
            cv, cvk = self.T_(2)
            hl = halo[:, (h * 3 + j) * 3:(h * 3 + j) * 3 + 3]
            self.CP("pool", cv[:, 0:3], hl, ["halo%d" % h], cvk)
            self.CP("act", cv[:, 3:3 + TT], pb[:, 0:TT], [pk], cvk)
            self.CP("pool", hl, cv[:, TT:TT + 3], cvk, ["halo%d" % h])
            acc, acck = self.T_()
            b = j * 8 + h
            self.TS("dve", acc, cv[:, 3:3 + TT], convw(3, b), None, ALU.mult, None, cvk + ["par"], acck)
            for tap in (2, 1, 0):
                self.STT(acc, cv[:, tap:tap + TT], convw(tap, b), acc, ALU.mult, ALU.add, cvk + acck + ["par"], acck)
            self.ACT(acc, acc, AF.Silu, acck, acck)
            act.append((acc, acck))
        (qs, qsk), (ks, ksk), (vs, vsk) = act
        pz, pzk = self.bank("proj")
        self.proj_block(wv, wk, 3, hnT, ["mix0"], pz, pzk)
        zs, zsk = self.T_()
        self.ACT(zs, pz[:, 0:TT], AF.Silu, [pzk], zsk)
        for (a, ak, scl) in ((qs, qsk, 128.0), (ks, ksk, 1.0)):
            sq, sqk = self.T_()
            self.ACT(sq, a, AF.Square, ak, sqk)
            pb, pk = self.bank("mix")
            self.MM(pb[:, 0:TT], C["ones128"], sq, sqk + ["cf"], [pk])
            self.RSQRT(sq, pb[:, 0:TT], scl, scl * 1e-6, [pk], sqk)
            self.TTo("dve", a, a, sq, ALU.mult, ak + sqk, ak)
        bbc, bbk = self.T_()
        gbc, gbk = self.T_()
        self.bcast_row(ga, gak, h, bbc, bbk)
        self.bcast_row(ga, gak, 8 + h, gbc, gbk)
        gam, gamk = self.T_()
        self.ACT(gam, gbc, AF.Exp, gbk, gamk)
        kb, kbk = self.T_()
        self.TTo("dve", kb, ks, bbc, ALU.mult, ksk + bbk, kbk)
        vb, vbk = self.T_()
        self.TTo("pool", vb, vs, bbc, ALU.mult, vsk + bbk, vbk)
        kbg, kbgk = self.T_()
        self.TTo("dve", kbg, kb, gam, ALU.mult, kbk + gamk, kbgk)
        qg, qgk = self.T_()
        self.TTo("pool", qg, qs, gam, ALU.mult, qsk + gamk, qgk)
        kd, kdk = self.T_()
        g3 = gbc.rearrange("p (c j) -> p c j", j=64)
        self.TTo("dve", kd.rearrange("p (c j) -> p c j", j=64), g3[:, :, 63:64].to_broadcast([128, TT // 64, 64]), g3,
                 ALU.subtract, gbk, kdk)
        self.ACT(kd, kd, AF.Exp, kdk, kdk)
        self.TTo("dve", kd, kd, ks, ALU.mult, kdk + ksk, kdk)
        po, pok = self.bank("acc")
        for tb in range(NTB):
            c0 = tb * 128
            sl = slice(c0, c0 + 128)
            gcol = gacol[:, tb * 32 + 8 + h:tb * 32 + 9 + h]
            ngcol = gacol[:, tb * 32 + 24 + h:tb * 32 + 25 + h]
            pa, pak = self.bank("mix")
            self.MM(pa[:, 0:128], kb[:, sl], ks[:, sl], kbk + ksk, [pak])
            self.MM(pa[:, 128:256], ks[:, sl], kb[:, sl], kbk + ksk, [pak])
            self.MM(pa[:, 256:384], ks[:, sl], qs[:, sl], qsk + ksk, [pak])
            ets, etsk = self.Q_()
            est, estk = self.Q_()
            self.ACT(ets, gbc[:, sl], AF.Exp, gbk + gck, etsk, scale=-1.0, bias=gcol)
            self.ACT(est, gbc[:, sl], AF.Exp, gbk + gck, estk, scale=1.0, bias=ngcol)
            self.STT(ets, ets, 1.0, C["m_ts_sneg64"], ALU.min, ALU.mult, etsk + ["cf"], etsk)
            self.STT(est, est, 1.0, C["m_st_incl64"], ALU.min, ALU.mult, estk + ["cf"], estk)
            X, Xk = self.Q_()
            XT, XTk = self.Q_()
            aqk, aqkk = self.Q_()
            self.TTo("dve", X, pa[:, 0:128], ets, ALU.mult, [pak] + etsk, Xk)
            self.TTo("dve", aqk, pa[:, 256:384], est, ALU.mult, [pak] + estk, aqkk)
            self.TTo("pool", est, est, C["m_st_sneg64"], ALU.mult, estk + ["cf"], estk)
            self.TTo("dve", XT, pa[:, 128:256], est, ALU.mult, [pak] + estk, XTk)
            Tt, Tk = self.tri_inverse(X, Xk, XT, XTk)
            pt, ptk = self.bank("mix")
            self.TR(pt[:, 0:128], vb[:, sl], C["identf"], vbk + ["cf"], [ptk])
            self.TR(pt[:, 128:256], kbg[:, sl], C["identf"], kbgk + ["cf"], [ptk])
            self.TR(pt[:, 256:384], kd[:, sl], C["identf"], kdk + ["cf"], [ptk])
            vbt, vbtk = self.Q_()
            kbgt, kbgtk = self.Q_()
            kdt, kdtk = self.Q_()
            self.CP("act", vbt, pt[:, 0:128], [ptk], vbtk)
            self.CP("dve", kbgt, pt[:, 128:256], [ptk], kbgtk)
            self.CP("act", kdt, pt[:, 256:384], [ptk], kdtk)
            pu, puk = self.bank("mix")
            self.MM(pu[:, 0:128], Tt, vbt, Tk + vbtk, [puk])
            self.MM(pu[:, 128:256], kbgt, Tt, Tk + kbgtk, [puk])
            u0, u0k = self.Q_()
            wT, wTk = self.Q_()
            self.CP("act", u0, pu[:, 0:128], [puk], u0k)
            self.CP("dve", wT, pu[:, 128:256], [puk], wTk)
            u, uk = self.Q_()
            for c in range(2):
                rs = slice(c * 64, c * 64 + 64)
                cs = slice(c0 + c * 64, c0 + c * 64 + 64)
                pw, pwk = self.bank("mix")
                self.MM(pw[rs, 0:128], wT[:, rs], S, wTk + Sk, [pwk])
                self.TTo("dve", u[rs, :], u0[rs, :], pw[rs, 0:128], ALU.subtract, u0k + [pwk], uk)
                self.MM(po[:, cs], S, qg[:, cs], Sk + qgk, [pok], start=True, stop=False)
                self.MM(po[:, cs], u[rs, :], aqk[rs, rs], uk + aqkk, [pok], start=False, stop=True)
                self.MM(pw[:, 128:256], kdt[rs, :], u[rs, :], kdtk + uk, [pwk])
                cd = gam[:, c0 + c * 64 + 63:c0 + c * 64 + 64]
                self.STT(S, S, cd, pw[:, 128:256], ALU.mult, ALU.add, Sk + gamk + [pwk], Sk)
        self.head_out(po, pok, zs, zsk, dn_g, h, 1e-6)

    def head_out(self, po, pok, zs, zsk, gcol, hidx, eps):
        C = self.C
        o, ok = self.T_()
        sq, sqk = self.T_()
        self.CP("act", o, po[:, 0:TT], [pok], ok)
        self.TTo("pool", sq, o, o, ALU.mult, ok, sqk)
        pb, pk = self.bank("mix")
        self.MM(pb[:, 0:TT], C["ones128"], sq, sqk + ["cf"], [pk])
        self.RSQRT(sq, pb[:, 0:TT], 1.0 / 128, eps, [pk], sqk)
        self.STT(o, o, gcol, sq, ALU.mult, ALU.mult, ok + sqk + ["par"], ok)
        self.TTo("dve", self.ogT[:, hidx, :], o, zs, ALU.mult, ok + zsk, ["A8_%d" % (hidx // 8)])

    def hg_head(self, e, h, tt, hnT, lb, oml, hg_g, Sh):
        C = self.C
        wb = self.win_bf[e]
        wv, wk = self.load_wblocks([wb[32 + h], wb[40 + h], wb[48 + h], wb[56 + h]], "win_bf%d" % e)
        S = Sh[:, h, :]
        Sk = ["Sh%d" % h]
        pq, pqk = self.bank("proj")
        self.proj_block(wv, wk, 0, hnT, ["mix0"], pq, pqk)
        q, qk = self.T_()
        self.ACT(q, pq[:, 0:TT], AF.Silu, [pqk], qk)
        pf, pfk = self.bank("proj")
        self.proj_block(wv, wk, 1, hnT, ["mix0"], pf, pfk)
        fg, fgk = self.T_()
        self.ACT(fg, pf[:, 0:TT], AF.Sigmoid, [pfk], fgk)
        self.TS("dve", fg, fg, oml[:, h:h + 1], lb[:, h:h + 1], ALU.mult, ALU.add, fgk + ["par"], fgk)
        k, kk_ = self.T_()
        self.TS("pool", k, fg, -1.0, 1.0, ALU.mult, ALU.add, fgk, kk_)
        lf, lfk = self.T_()
        self.ACT(lf, fg, AF.Ln, fgk, lfk)
        bl, blk = self.T_()
        self.SCAN(bl, self.reset32, lf, lfk + ["cm"], blk)
        eb, ebk = self.T_()
        enb, enbk = self.T_()
        self.ACT(eb, bl, AF.Exp, blk, ebk)
        self.ACT(enb, bl, AF.Exp, blk, enbk, scale=-1.0)
        self.STT(q, q, 128.0 ** -0.5, eb, ALU.mult, ALU.mult, qk + ebk, qk)
        self.TTo("dve", k, k, enb, ALU.mult, kk_ + enbk, kk_)
        kh, khk = self.T_()
        e3 = eb.rearrange("p (c j) -> p c j", j=32)
        self.TTo("dve", kh.rearrange("p (c j) -> p c j", j=32), k.rearrange("p (c j) -> p c j", j=32),
                 e3[:, :, 31:32].to_broadcast([128, TT // 32, 32]), ALU.mult, kk_ + ebk, khk)
        pv, pvk = self.bank("proj")
        self.proj_block(wv, wk, 2, hnT, ["mix0"], pv, pvk)
        v, vk = self.T_()
        self.CP("act", v, pv[:, 0:TT], [pvk], vk)
        pz, pzk = self.bank("proj")
        self.proj_block(wv, wk, 3, hnT, ["mix0"], pz, pzk)
        zs, zsk = self.T_()
        self.ACT(zs, pz[:, 0:TT], AF.Silu, [pzk], zsk)
        po, pok = self.bank("acc")
        for tb in range(NTB):
            c0 = tb * 128
            sl = slice(c0, c0 + 128)
            pa, pak = self.bank("mix")
            self.MM(pa[:, 0:128], k[:, sl], q[:, sl], kk_ + qk, [pak])
            self.TR(pa[:, 128:256], v[:, sl], C["identf"], vk + ["cf"], [pak])
            self.TR(pa[:, 256:384], kh[:, sl], C["identf"], khk + ["cf"], [pak])
            At, Atk = self.Q_()
            vt, vtk = self.Q_()
            kht, khtk = self.Q_()
            self.TTo("dve", At, pa[:, 0:128], C["m_st_incl32"], ALU.mult, [pak, "cf"], Atk)
            self.CP("act", vt, pa[:, 128:256], [pak], vtk)
            self.CP("act", kht, pa[:, 256:384], [pak], khtk)
            self.MM(po[:, sl], vt, At, vtk + Atk, [pok], start=True, stop=False)
            for c in range(4):
                cs = slice(c0 + c * 32, c0 + c * 32 + 32)
                self.MM(po[:, cs], S, q[:, cs], Sk + qk, [pok], start=False, stop=(c == 3))
                khm, khmk = self.Q_()
                self.TS("pool", khm, kht, self.rowmask32[:, c:c + 1], None, ALU.mult, None, khtk + ["cm"], khmk)
                pw, pwk = self.bank("mix")
                self.MM(pw[:, 0:128], khm, vt, khmk + vtk, [pwk])
                lam = eb[:, c0 + c * 32 + 31:c0 + c * 32 + 32]
                self.STT(S, S, lam, pw[:, 0:128], ALU.mult, ALU.add, Sk + ebk + [pwk], Sk)
        self.head_out(po, pok, zs, zsk, hg_g, 8 + h, 1e-6)

    def odd_layer(self, layer):
        o = layer // 2
        I, C = self.I, self.C
        first = (o == 0)
        self.load_gain(I["norm_gains"][layer:layer + 1, :])
        par = self.par_t
        R = self.load_pp([I["rw_mu"][o].rearrange("i (b p) -> (i b) p", p=128),
                          I["rw_w0"][o].rearrange("(b p) -> b p", p=128), I["rw_a0"][o].rearrange("(b p) -> b p", p=128)])
        self.CP("dve", par[:, 0:128], self.pp[:, 0:128], ["pp"], ["par"])
        rows = [I["rw_k_k"][o], I["rw_k_a"][o], I["rw_r_k"][o], I["rw_ln_w"][o], I["rw_ln_b"][o], I["rw_v0"][0]]
        R = self.load_pp([r_.rearrange("(b p) -> b p", p=128) for r_ in rows])
        self.CP("dve", par[:, 128:224], self.pp[:, 0:96], ["pp"], ["par"])
        mu = lambda i, b: par[:, i * 16 + b:i * 16 + b + 1]
        w0 = lambda b: par[:, 96 + b:97 + b]
        a0 = lambda b: par[:, 112 + b:113 + b]
        k_k = lambda b: par[:, 128 + b:129 + b]
        k_a = lambda b: par[:, 144 + b:145 + b]
        r_k = lambda b: par[:, 160 + b:161 + b]
        ln_w = lambda b: par[:, 176 + b:177 + b]
        ln_b = lambda b: par[:, 192 + b:193 + b]
        v0 = lambda b: par[:, 208 + b:209 + b]
        omka = lambda b: par[:, 224 + b:225 + b]
        self.TS("dve", par[:, 224:240], par[:, 144:160], -1.0, 1.0, ALU.mult, ALU.add, ["par"], ["par"])
        L = self.lora
        w1b = L[:, 0:1536].rearrange("p (k c) -> p k c", c=96)
        a1b = L[:, 1536:3072].rearrange("p (k c) -> p k c", c=96)
        v1b = L[:, 3072:4096].rearrange("p (k c) -> p k c", c=64)
        w2b = L[:, 4096:6144]
        a2b = L[:, 6144:8192]
        v2b = L[:, 8192:10240]
        def ld_lora(dst, src, npart, ncols, is3):
            stg, sk = self.T_(8)
            if is3:
                self.DMA("sp", stg[:, 0:ncols].rearrange("p (k c) -> p k c", k=16), src.rearrange("(k p) c -> p k c", p=128), [], sk)
                self.CP("dve", dst, stg[:, 0:ncols].rearrange("p (k c) -> p k c", k=16), sk, ["lora"])
            else:
                self.DMA("sp", stg[0:npart, 0:ncols], src, [], sk)
                self.CP("dve", dst[0:npart, :], stg[0:npart, 0:ncols], sk, ["lora"])
        ld_lora(w1b, I["rw_w1"][o], 128, 1536, True)
        ld_lora(a1b, I["rw_a1"][o], 128, 1536, True)
        ld_lora(w2b, I["rw_w2"][o], 96, 2048, False)
        ld_lora(a2b, I["rw_a2"][o], 96, 2048, False)
        if not first:
            ld_lora(v1b, I["rw_v1"][0], 128, 1024, True)
            ld_lora(v2b, I["rw_v2"][0], 64, 2048, False)
        M = self.stS[:, 0:1024].rearrange("p (b v) -> p b v", v=64)
        self.MEMSET("pool", self.stS[:, :], 0.0, ["M%d" % b for b in range(16)])
        hnT = self.hnTo
        self.MEMSET("pool", hnT[:, :, 0:1], 0.0, ["hnTo"])
        mixes = [self.mixb[:, i, 0:16 * TT].rearrange("p (k t) -> p k t", t=TT) for i in range(6)]
        mk_ = ["mix%d" % i for i in range(6)]
        wr = self.wr_bf[o]
        for tt in range(self.NT):
            t0 = tt * TT
            if tt > 0:
                self.CP("pool", hnT[:, :, 0:1], hnT[:, :, TT:TT + 1], ["hnTo"], ["hnTo"])
            self.load_norm_T(layer, tt, hnT, ["hnTo"], 1)
            for k in range(16):
                xx, xxk = self.T_()
                self.TTo("pool", xx, hnT[:, k, 0:TT], hnT[:, k, 1:TT + 1], ALU.subtract, ["hnTo"], xxk)
                for i in range(6):
                    self.STT(mixes[i][:, k, :], xx, mu(i, k), hnT[:, k, 1:TT + 1], ALU.mult, ALU.add,
                             xxk + ["hnTo", "par"], [mk_[i]])
            hidk = ["ga_t", "gacol_t"]
            hb = self.ga_t[:, :].bitcast(BF16)
            hwb, hab = hb[:, 0:TT], hb[:, TT:2 * TT]
            hvb = self.gacol_t[:, :].bitcast(BF16)[:, 0:TT]
            for (wt, mi, dst, n, fn) in ((w1b, 1, hwb, 96, AF.Tanh), (a1b, 4, hab, 96, AF.Copy), (v1b, 3, hvb, 64, AF.Copy)):
                if first and mi == 3:
                    continue
                pb, pk = self.bank("proj")
                for k in range(16):
                    self.MM(pb[0:n, 0:TT], wt[:, k, :], mixes[mi][:, k, :], ["lora", mk_[mi]], [pk], start=(k == 0), stop=(k == 15))
                self.ACT(dst[0:n, :], pb[0:n, 0:TT], fn, [pk], hidk)
            for j in range(16):
                self.rw_block(o, layer, tt, j, first, wr, mixes, mk_, hwb, hab, hvb, hidk, w2b, a2b, v2b,
                              w0, a0, v0, k_k, k_a, omka, r_k, ln_w, ln_b, M)
            self.out_proj(layer, tt)

    def rw_block(self, o, layer, tt, j, first, wr, mixes, mk_, hwb, hab, hvb, hidk, w2b, a2b, v2b,
                 w0, a0, v0, k_k, k_a, omka, r_k, ln_w, ln_b, M):
        C = self.C
        t0 = tt * TT
        bs = slice(j * 128, (j + 1) * 128)
        wv, wk = self.load_wblocks([wr[0, j], wr[1, j], wr[2, j], wr[3, j]], "wr_bf%d" % o)
        outs = []
        for i, mi in enumerate((0, 2, 3, 5)):
            pb, pk = self.bank("proj")
            self.proj_block(wv, wk, i, mixes[mi], [mk_[mi]], pb, pk)
            t_, tk = self.T_()
            if i == 3:
                self.ACT(t_, pb[:, 0:TT], AF.Silu, [pk], tk)
            else:
                self.CP("act" if i != 1 else "dve", t_, pb[:, 0:TT], [pk], tk)
            outs.append((t_, tk))
        (r, rk_), (k, kk_), (v, vk), (zs, zsk) = outs
        ld, ldk = self.T_()
        pb, pk = self.bank("proj")
        self.MM(pb[:, 0:TT], w2b[0:96, bs], hwb[0:96, :], ["lora"] + hidk, [pk])
        self.ACT(ld, pb[:, 0:TT], AF.Sigmoid, [pk, "par"], ldk, bias=w0(j))
        self.TS("pool", ld, ld, -0.6065306597126334, None, ALU.mult, None, ldk, ldk)
        a, ak = self.T_()
        pb, pk = self.bank("proj")
        self.MM(pb[:, 0:TT], a2b[0:96, bs], hab[0:96, :], ["lora"] + hidk, [pk])
        self.ACT(a, pb[:, 0:TT], AF.Sigmoid, [pk, "par"], ak, bias=a0(j))
        if first:
            self.DMA("sp", self.vfD[bs, t0:t0 + TT], v, vk, ["vf%d_%d" % (tt, j)])
        else:
            sg, sgk = self.T_()
            vf, vfk = self.T_()
            pb, pk = self.bank("proj")
            self.MM(pb[:, 0:TT], v2b[0:64, bs], hvb[0:64, :], ["lora"] + hidk, [pk])
            self.ACT(sg, pb[:, 0:TT], AF.Sigmoid, [pk, "par"], sgk, bias=v0(j))
            self.DMA("sp", vf, self.vfD[bs, t0:t0 + TT], ["vf%d_%d" % (tt, j)], vfk)
            self.TTo("dve", vf, vf, v, ALU.subtract, vfk + vk, vfk)
            self.TTo("dve", vf, vf, sg, ALU.mult, vfk + sgk, vfk)
            self.TTo("dve", v, v, vf, ALU.add, vfk + vk, vk)
        kk, kkk = self.T_()
        sq, sqk = self.T_()
        self.TS("dve", kk, k, k_k(j), None, ALU.mult, None, kk_ + ["par"], kkk)
        self.ACT(sq, kk, AF.Square, kkk, sqk)
        pb, pk = self.bank("mix")
        self.MM(pb[:, 0:TT], C["ones64bd"], sq, sqk + ["cf"], [pk])
        self.RSQRT(sq, pb[:, 0:TT], 1.0, 1e-6, [pk], sqk)
        self.TTo("dve", kk, kk, sq, ALU.mult, kkk + sqk, kkk)
        k2, k2k = self.T_()
        self.TS("dve", k2, a, k_a(j), omka(j), ALU.mult, ALU.add, ak + ["par"], k2k)
        self.TTo("dve", k2, k2, k, ALU.mult, k2k + kk_, k2k)
        bsum, bsk = self.T_()
        self.STT(sq, r, r_k(j), k2, ALU.mult, ALU.mult, rk_ + k2k + ["par"], sqk)
        pb, pk = self.bank("mix")
        self.MM(pb[:, 0:TT], C["ones64bd"], sq, sqk + ["cf"], [pk])
        self.TTo("dve", bsum, pb[:, 0:TT], v, ALU.mult, [pk] + vk, bsk)
        bt, btk = self.T_()
        self.TTo("pool", bt, kk, a, ALU.mult, kkk + ak, btk)
        cl, clk = self.T_()
        self.SCAN(cl, self.reset64, ld, ldk + ["cm"], clk)
        ep, epk = self.T_()
        en, enk = self.T_()
        eex, eexk = self.T_()
        self.ACT(ep, cl, AF.Exp, clk, epk)
        self.ACT(en, cl, AF.Exp, clk, enk, scale=-1.0)
        self.TTo("pool", eex, cl, ld, ALU.subtract, clk + ldk, eexk)
        self.ACT(eex, eex, AF.Exp, eexk, eexk)
        rt, rtk = self.T_()
        at, atk = self.T_()
        self.TTo("dve", rt, r, ep, ALU.mult, rk_ + epk, rtk)
        self.STT(at, kk, -1.0, eex, ALU.mult, ALU.mult, kkk + eexk, atk)
        self.TTo("dve", bt, bt, en, ALU.mult, btk + enk, btk)
        self.TTo("dve", k2, k2, en, ALU.mult, k2k + enk, k2k)
        kt, ktk = k2, k2k
        bh, bhk = self.T_()
        kh, khk = self.T_()
        e3 = ep.rearrange("p (c q) -> p c q", q=64)
        pcb = e3[:, :, 63:64].to_broadcast([128, TT // 64, 64])
        self.TTo("dve", bh.rearrange("p (c q) -> p c q", q=64), bt.rearrange("p (c q) -> p c q", q=64), pcb, ALU.mult, btk + epk, bhk)
        self.TTo("dve", kh.rearrange("p (c q) -> p c q", q=64), kt.rearrange("p (c q) -> p c q", q=64), pcb, ALU.mult, ktk + epk, khk)
        po, pok = self.bank("acc")
        for tb in range(NTB):
            c0 = tb * 128
            sl = slice(c0, c0 + 128)
            tok, tokk = self.T_(2)
            pt, ptk = self.bank("mix")
            for q_, (src, srck) in enumerate(((at, atk), (bh, bhk), (kh, khk), (v, vk))):
                self.TR(pt[:, q_ * 128:(q_ + 1) * 128], src[:, sl], C["identf"], srck + ["cf"], [ptk])
            self.CP("act", tok, pt[:, 0:512], [ptk], tokk)
            atok, bhtok, khtok, vtok = (tok[:, q_ * 128:(q_ + 1) * 128] for q_ in range(4))
            for hh in range(2):
                hs = slice(hh * 64, hh * 64 + 64)
                Mh = M[hs, j, :]
                Mk = ["M%d" % j]
                pa, pak = self.bank("mix")
                self.MM(pa[:, 0:128], at[hs, sl], bt[hs, sl], atk + btk, [pak])
                self.MM(pa[:, 128:256], bt[hs, sl], at[hs, sl], atk + btk, [pak])
                self.MM(pa[:, 256:384], kt[hs, sl], at[hs, sl], atk + ktk, [pak])
                self.MM(pa[:, 384:512], bt[hs, sl], rt[hs, sl], rtk + btk, [pak])
                pa2, pa2k = self.bank("mix")
                self.MM(pa2[:, 0:128], kt[hs, sl], rt[hs, sl], rtk + ktk, [pa2k])
                X, Xk = self.Q_()
                XT, XTk = self.Q_()
                Aak, Aakk = self.Q_()
                Arb, Arbk = self.Q_()
                Ark, Arkk = self.Q_()
                self.TTo("dve", X, pa[:, 0:128], C["m_ts_strict64"], ALU.mult, [pak, "cf"], Xk)
                self.TTo("dve", XT, pa[:, 128:256], C["m_st_strict64"], ALU.mult, [pak, "cf"], XTk)
                self.TTo("dve", Aak, pa[:, 256:384], C["m_st_strict64"], ALU.mult, [pak, "cf"], Aakk)
                self.TTo("dve", Arb, pa[:, 384:512], C["m_st_incl64"], ALU.mult, [pak, "cf"], Arbk)
                self.TTo("dve", Ark, pa2[:, 0:128], C["m_st_incl64"], ALU.mult, [pa2k, "cf"], Arkk)
                Tt, Tk = self.tri_inverse(X, Xk, XT, XTk)
                pu, puk = self.bank("mix")
                self.MM(pu[:, 0:64], Aak, vtok[:, hs], Aakk + tokk, [puk])
                AV, AVk = self.Q_()
                self.CP("act", AV[:, 0:64], pu[:, 0:64], [puk], AVk)
                pu2, pu2k = self.bank("mix")
                self.MM(pu2[:, 0:64], Tt, AV[:, 0:64], Tk + AVk, [pu2k])
                self.MM(pu2[hs, 128:256], atok[:, hs], Tt, Tk + tokk, [pu2k])
                U0, U0k = self.Q_()
                self.CP("act", U0[:, 0:64], pu2[:, 0:64], [pu2k], U0k)
                self.CP("act", U0[hs, 64:128], pu2[hs, 128:192], [pu2k], U0k)
                WtT, WtTk = self.Q_()
                self.CP("dve", WtT[hs, :], pu2[hs, 128:256], [pu2k], WtTk)
                U, Uk = self.Q_()
                for c in range(2):
                    rs = slice(c * 64, c * 64 + 64)
                    cs = slice(c0 + c * 64, c0 + c * 64 + 64)
                    pw, pwk = self.bank("mix")
                    self.MM(pw[rs, 0:64], WtT[hs, rs], Mh, WtTk + Mk, [pwk])
                    self.TTo("dve", U[rs, 0:64], U0[rs, 0:64], pw[rs, 0:64], ALU.add, U0k + [pwk], Uk)
                    self.MM(po[hs, cs], Mh, rt[hs, cs], Mk + rtk, [pok], start=True, stop=False)
                    self.MM(po[hs, cs], U[rs, 0:64], Arb[rs, rs], Uk + Arbk, [pok], start=False, stop=False)
                    self.MM(po[hs, cs], vtok[rs, hs], Ark[rs, rs], tokk + Arkk, [pok], start=False, stop=True)
                    self.MM(pw[hs, 128:192], bhtok[rs, hs], U[rs, 0:64], tokk + Uk, [pwk], start=True, stop=False)
                    self.MM(pw[hs, 128:192], khtok[rs, hs], vtok[rs, hs], tokk, [pwk], start=False, stop=True)
                    pc = ep[hs, c0 + c * 64 + 63:c0 + c * 64 + 64]
                    self.STT(Mh, Mh, pc, pw[hs, 128:192], ALU.mult, ALU.add, Mk + epk + [pwk], Mk)
        y, yk = self.T_()
        d, dk_ = self.T_()
        self.CP("act", y, po[:, 0:TT], [pok], yk)
        pb, pk = self.bank("mix")
        self.MM(pb[:, 0:TT], C["ones64bd"], y, yk + ["cf"], [pk])
        self.STT(d, pb[:, 0:TT], -1.0 / 64, y, ALU.mult, ALU.add, [pk] + yk, dk_)
        self.TTo("pool", y, d, d, ALU.mult, dk_, yk)
        pb, pk = self.bank("mix")
        self.MM(pb[:, 0:TT], C["ones64bd"], y, yk + ["cf"], [pk])
        self.RSQRT(y, pb[:, 0:TT], 1.0 / 64, 64e-5, [pk], yk)
        self.STT(d, d, ln_w(j), y, ALU.mult, ALU.mult, dk_ + yk + ["par"], dk_)
        self.STT(d, d, ln_b(j), bsum, ALU.add, ALU.add, dk_ + bsk + ["par"], dk_)
        self.TTo("dve", self.ogT[:, j, :], d, zs, ALU.mult, dk_ + zsk, ["A8_%d" % (j // 8)])


def build_program(T, nlayers, final_norm=True):
    b = Builder(T, nlayers, final_norm)
    return b.build()


_ORDER = ["x", "norm_gains", "mix_w_in", "dn_conv", "dn_a_log", "dn_dt_bias", "dn_out_norm", "hg_lb_logits",
          "hg_out_norm", "mix_w_out", "rw_mu", "rw_w_rkvz", "rw_w0", "rw_w1", "rw_w2", "rw_a0", "rw_a1", "rw_a2",
          "rw_v0", "rw_v1", "rw_v2", "rw_k_k", "rw_k_a", "rw_r_k", "rw_ln_w", "rw_ln_b", "rw_w_out", "final_norm"]


def make_in_maps(inputs, cores, T):
    hc = host_consts()
    shared = {}
    for k in _ORDER:
        if k == "x":
            continue
        a = np.ascontiguousarray(np.asarray(inputs[k], dtype=np.float32))
        if k == "rw_r_k":
            a = a.reshape(2, D)
        if k == "final_norm":
            a = a.reshape(1, D)
        shared[k] = a
    shared["identb"] = hc["identb"]
    shared["cf"] = hc["cf"]
    shared["cm"] = hc["cm"]
    maps = []
    x = np.asarray(inputs["x"], dtype=np.float32)
    for c in cores:
        m = dict(shared)
        m["x"] = np.ascontiguousarray(x[c, :T])
        maps.append(m)
    return maps


def kernel(**inputs):
    nc = build_program(SEQ, 4)
    maps = make_in_maps(inputs, list(range(NB)), SEQ)
    res = run_bass_kernel_spmd(nc, maps, core_ids=list(range(NB)))
    return np.stack([np.asarray(r["y"], dtype=np.float32) for r in res.results], axis=0)
```
